# Optimizing a Trainium2 kernel written in Bass

```python
import jax
import jax.numpy as jnp
from jax import lax
import numpy as np

D_MODEL = 2048
BATCH = 4
SEQ = 4096
DEPTH = 2

CHUNK = 64
N_MEM = 256
N_BRANCH = 4
BRANCH_WIDTH = D_MODEL // 4
HEAD_DIM = 128
N_HEADS = BRANCH_WIDTH // HEAD_DIM
IDX_HEADS = 8
IDX_DIM = 64
TOPK_MAX = 256
Q_BLOCK = 128
ROPE_BASE = 10000.0
EPS = 1e-6
W = BRANCH_WIDTH
SPLIT_SIZES = (W, W, W, W, IDX_HEADS * IDX_DIM, IDX_DIM, IDX_HEADS,
               W, W, W, W,
               W, W, W, W,
               W, W,
               N_BRANCH * D_MODEL)
IN_WIDTH = sum(SPLIT_SIZES)

kernel_name = 'hybrid_dsa_retnet_hgrn2_gated_block'


def rms_norm(x, g):
    xf = x.astype(jnp.float32)
    y = xf * lax.rsqrt(jnp.mean(xf * xf, axis=-1, keepdims=True) + EPS)
    return (y * g.astype(jnp.float32)).astype(x.dtype)


def head_rms_norm(t, g):
    tf = t.astype(jnp.float32)
    y = tf * lax.rsqrt(jnp.mean(tf * tf, axis=-1, keepdims=True) + EPS)
    return (y * g.astype(jnp.float32)).astype(t.dtype)


def split_cols(u):
    parts = []
    off = 0
    for n in SPLIT_SIZES:
        parts.append(u[..., off:off + n])
        off += n
    return parts


def rope_tables(seq):
    inv = ROPE_BASE ** (-jnp.arange(0, HEAD_DIM, 2, dtype=jnp.float32) / HEAD_DIM)
    ang = jnp.arange(seq, dtype=jnp.float32)[:, None] * inv[None, :]
    return jnp.cos(ang), jnp.sin(ang)


def rope(t, cos, sin):
    half = HEAD_DIM // 2
    t1 = t[..., :half].astype(jnp.float32)
    t2 = t[..., half:].astype(jnp.float32)
    c, s = cos[:, None, :], sin[:, None, :]
    return jnp.concatenate([t1 * c - t2 * s, t1 * s + t2 * c], axis=-1).astype(t.dtype)


def sparse_index_attention(q, k, v, qi, ki, wi):
    B, S, H, Dh = q.shape
    n_sel = min(TOPK_MAX, S // 4)
    nb = S // Q_BLOCK
    key_chunk = jnp.arange(S) // CHUNK

    def to_blocks(t):
        return jnp.moveaxis(t.reshape(B, nb, Q_BLOCK, *t.shape[2:]), 1, 0)

    def block(args):
        start, qb, qib, wib = args
        q_chunk = (start + jnp.arange(Q_BLOCK)) // CHUNK
        visible = key_chunk[None, :] <= q_chunk[:, None]
        rel = jax.nn.relu(jnp.einsum('bqhd,bsd->bqhs', qib, ki).astype(jnp.float32) * IDX_DIM ** -0.5)
        score = jnp.einsum('bqh,bqhs->bqs', wib.astype(jnp.float32) * IDX_HEADS ** -0.5, rel)
        score = jnp.where(visible[None], score, -jnp.inf)
        _, sel = lax.top_k(score, n_sel)
        sel_ok = key_chunk[sel] <= q_chunk[None, :, None]
        ks = jax.vmap(lambda kk, ii: kk[ii])(k, sel)
        vs = jax.vmap(lambda vv, ii: vv[ii])(v, sel)
        logits = jnp.einsum('bqhd,bqkhd->bhqk', qb, ks).astype(jnp.float32) * HEAD_DIM ** -0.5
        logits = jnp.where(sel_ok[:, None], logits, -jnp.inf)
        p = jax.nn.softmax(logits, axis=-1).astype(vs.dtype)
        return jnp.einsum('bhqk,bqkhd->bqhd', p, vs)

    starts = jnp.arange(nb) * Q_BLOCK
    out = lax.map(block, (starts, to_blocks(q), to_blocks(qi), to_blocks(wi)))
    return jnp.moveaxis(out, 0, 1).reshape(B, S, H, Dh)


def retention(q, k, v):
    B, S, H, d = q.shape
    nc = S // CHUNK
    q = q.astype(jnp.float32).reshape(B, nc, CHUNK, H, d)
    k = k.astype(jnp.float32).reshape(B, nc, CHUNK, H, d)
    v = v.astype(jnp.float32).reshape(B, nc, CHUNK, H, d)
    log_g = jnp.log1p(-jnp.exp2(-5.0 - jnp.arange(H, dtype=jnp.float32)))
    pos = jnp.arange(CHUNK, dtype=jnp.float32)
    intra = jnp.exp(log_g[:, None, None] * jnp.abs(pos[:, None] - pos[None, :]))
    s = jnp.einsum('bnchd,bnehd->bnhce', q, k) * intra
    o_intra = jnp.einsum('bnhce,bnehd->bnchd', s, v)
    k_dec = k * jnp.exp(log_g[None, :] * (CHUNK - 1 - pos)[:, None])[:, :, None]
    kv = jnp.einsum('bnehd,bnehv->nbhdv', k_dec, v)
    chunk_decay = jnp.exp(log_g * CHUNK)[None, :, None, None]

    def step(state, kv_n):
        return chunk_decay * state + kv_n, state

    _, s_prev = lax.scan(step, jnp.zeros_like(kv[0]), kv)
    q_dec = q * jnp.exp(log_g[None, :] * (pos + 1.0)[:, None])[:, :, None]
    o_inter = jnp.einsum('bnchd,nbhdv->bnchv', q_dec, s_prev)
    return (o_intra + o_inter).reshape(B, S, H, d)


def hgrn2(f, i, q):
    B, S, H, d = q.shape
    nc = S // CHUNK
    f = f.astype(jnp.float32).reshape(B, nc, CHUNK, H, d)
    i = i.astype(jnp.float32).reshape(B, nc, CHUNK, H, d)
    q = q.astype(jnp.float32).reshape(B, nc, CHUNK, H, d)
    kin = 1.0 - f

    def tmaj(t):
        return jnp.moveaxis(t, 2, 0)

    def step(state, inp):
        f_t, k_t, i_t, q_t = inp
        state = f_t[..., None] * state + k_t[..., None] * i_t[..., None, :]
        return state, jnp.einsum('bnhd,bnhdv->bnhv', q_t, state)

    s0 = jnp.zeros((B, nc, H, d, d), jnp.float32)
    s_local, o_intra = lax.scan(step, s0, (tmaj(f), tmaj(kin), tmaj(i), tmaj(q)))
    o_intra = jnp.moveaxis(o_intra, 0, 2)
    cum = jnp.cumsum(jnp.log(f), axis=2)
    chunk_decay = jnp.exp(cum[:, :, -1])

    def cstep(state, inp):
        dec, sl = inp
        return dec[..., None] * state + sl, state

    _, s_prev = lax.scan(cstep, jnp.zeros((B, H, d, d), jnp.float32),
                         (jnp.moveaxis(chunk_decay, 1, 0), jnp.moveaxis(s_local, 1, 0)))
    o_inter = jnp.einsum('bnchd,nbhdv->bnchv', q * jnp.exp(cum), s_prev)
    return (o_intra + o_inter).reshape(B, S, H, d)


def memory_attention(q, mk, mv):
    logits = jnp.einsum('bshd,bmhd->bhsm', q, mk).astype(jnp.float32) * HEAD_DIM ** -0.5
    p = jax.nn.softmax(logits, axis=-1).astype(mv.dtype)
    return jnp.einsum('bhsm,bmhd->bshd', p, mv)


def hybrid_layer(x, mem, norm_g, w_in, qk_g, ret_g, hgrn_g, lb, w_mem_kv, w_branch, w_out, cos, sin):
    B, S, _ = x.shape
    dt = x.dtype
    h = rms_norm(x, norm_g)
    u = h @ w_in
    (a_q, a_k, a_v, a_z, a_qi, a_ki, a_wi,
     b_q, b_k, b_v, b_z,
     c_f, c_i, c_q, c_z,
     m_q, m_z, gates) = split_cols(u)

    def heads(t):
        return t.reshape(t.shape[0], t.shape[1], N_HEADS, HEAD_DIM)

    y_a = sparse_index_attention(head_rms_norm(heads(a_q), qk_g[0]), head_rms_norm(heads(a_k), qk_g[1]),
                                 heads(a_v), a_qi.reshape(B, S, IDX_HEADS, IDX_DIM), a_ki, a_wi)
    y_b = retention(rope(heads(b_q), cos, sin), rope(heads(b_k), cos, sin) * HEAD_DIM ** -0.5, heads(b_v))
    y_b = head_rms_norm(y_b, ret_g)
    f = lb + (1.0 - lb) * jax.nn.sigmoid(heads(c_f).astype(jnp.float32))
    y_c = head_rms_norm(hgrn2(f, heads(c_i), heads(c_q)), hgrn_g)
    mkv = mem @ w_mem_kv
    m_k, m_v = mkv[..., :W], mkv[..., W:]
    y_m = memory_attention(head_rms_norm(heads(m_q), qk_g[2]), head_rms_norm(heads(m_k), qk_g[3]), heads(m_v))

    gate_all = gates.reshape(B, S, N_BRANCH, D_MODEL)
    merged = jnp.zeros_like(x)
    for n, (y, z) in enumerate(((y_a, a_z), (y_b, b_z), (y_c, c_z), (y_m, m_z))):
        branch = y.reshape(B, S, W).astype(dt) * jax.nn.silu(z)
        merged = merged + jax.nn.sigmoid(gate_all[:, :, n]) * (branch @ w_branch[n])
    return x + merged @ w_out


def setup_inputs(seed: int = 0) -> dict:
    key = jax.random.key(seed)
    ks = jax.random.split(key, 12)
    x = jax.random.normal(ks[0], (BATCH, SEQ, D_MODEL), jnp.float32)
    mem = jax.random.normal(ks[1], (BATCH, N_MEM, D_MODEL), jnp.float32)
    norm_g = 1.0 + 0.02 * jax.random.normal(ks[2], (DEPTH, D_MODEL), jnp.float32)
    w_in = jax.random.normal(ks[3], (DEPTH, D_MODEL, IN_WIDTH), jnp.float32) * D_MODEL ** -0.5
    qk_norm_g = 1.0 + 0.02 * jax.random.normal(ks[4], (DEPTH, 4, HEAD_DIM), jnp.float32)
    ret_norm_g = 1.0 + 0.02 * jax.random.normal(ks[5], (DEPTH, N_HEADS, HEAD_DIM), jnp.float32)
    hgrn_norm_g = 1.0 + 0.02 * jax.random.normal(ks[6], (DEPTH, N_HEADS, HEAD_DIM), jnp.float32)
    lb_logits = 0.5 * jax.random.normal(ks[7], (DEPTH, N_HEADS, HEAD_DIM), jnp.float32)
    w_mem_kv = jax.random.normal(ks[8], (DEPTH, D_MODEL, 2 * W), jnp.float32) * D_MODEL ** -0.5
    w_branch = jax.random.normal(ks[9], (DEPTH, N_BRANCH, W, D_MODEL), jnp.float32) * W ** -0.5
    w_out = jax.random.normal(ks[10], (DEPTH, D_MODEL, D_MODEL), jnp.float32) * D_MODEL ** -0.5
    return {'x': x, 'mem': mem, 'norm_g': norm_g, 'w_in': w_in, 'qk_norm_g': qk_norm_g,
            'ret_norm_g': ret_norm_g, 'hgrn_norm_g': hgrn_norm_g, 'lb_logits': lb_logits,
            'w_mem_kv': w_mem_kv, 'w_branch': w_branch, 'w_out': w_out}


def reference(x, mem, norm_g, w_in, qk_norm_g, ret_norm_g, hgrn_norm_g, lb_logits, w_mem_kv, w_branch, w_out):
    seq = x.shape[1]
    p = jax.nn.softmax(lb_logits.astype(jnp.float32), axis=0)
    lower_bounds = jnp.cumsum(p, axis=0) - p[0:1]
    cos, sin = rope_tables(seq)
    for l in range(DEPTH):
        x = hybrid_layer(x, mem, norm_g[l], w_in[l], qk_norm_g[l], ret_norm_g[l], hgrn_norm_g[l],
                         lower_bounds[l], w_mem_kv[l], w_branch[l], w_out[l], cos, sin)
    return x
```

```python
import types
import numpy as np
import ml_dtypes
from contextlib import ExitStack
import concourse.bass as bass
import concourse.mybir as mybir
from concourse.bass_utils import run_bass_kernel_spmd

F32 = mybir.dt.float32
BF16 = mybir.dt.bfloat16
AF = mybir.ActivationFunctionType
ALU = mybir.AluOpType
AX = mybir.AxisListType

D = 2048
W = 512
IDX_H = 8
IDX_D = 64
N_MEM = 256
EPS = 1e-6
IN_WIDTH = 15944
OFF = dict(a_q=0, a_k=512, a_v=1024, a_z=1536, a_qi=2048, a_ki=2560, a_wi=2624,
           b_q=2632, b_k=3144, b_v=3656, b_z=4168, c_f=4680, c_i=5192, c_q=5704, c_z=6216,
           m_q=6728, m_z=7240, gates=7752)
NITER = 18
VW = 130


def _freeze(fn):
    if fn is None or fn.__closure__ is None:
        return fn
    cells = []
    for c in fn.__closure__:
        try:
            cells.append(types.CellType(c.cell_contents))
        except ValueError:
            cells.append(c)
    return types.FunctionType(fn.__code__, fn.__globals__, fn.__name__, fn.__defaults__, tuple(cells))


class _Stop(Exception):
    pass


STOP_AT = None


def _ck(name):
    if STOP_AT == name:
        raise _Stop()


class Sched:
    def __init__(self, nc, es):
        self.nc, self.es = nc, es
        self.eng = dict(pe=nc.tensor, act=nc.scalar, dve=nc.vector, pool=nc.gpsimd, sp=nc.sync)
        self.prog = {e: [] for e in self.eng}
        self.cur = {}
        self.nsem = 0
        for e in self.eng:
            self.cur[e] = [self._newsem(e), 0]
        self.seen = {e: {} for e in self.eng}
        self.lw = {}
        self.rd = {}
        self.dsem = {}
        self.alltok = {}
        self.free_dsems = []

    def _newsem(self, nm):
        self.nsem += 1
        return self.es.enter_context(self.nc.semaphore(f"s{self.nsem}_{nm}"))

    def _deps(self, r, w):
        deps = {}

        def add(t):
            if t is not None and deps.get(id(t[0]), (None, -1))[1] < t[1]:
                deps[id(t[0])] = t

        for k in r:
            add(self.lw.get(k))
        for k in w:
            add(self.lw.get(k))
            for t in self.rd.get(k, {}).values():
                add(t)
        return deps

    def _waits(self, e, deps):
        out = []
        own = id(self.cur[e][0])
        for sid, t in deps.items():
            if e == 'pe' and sid == own:
                continue
            if self.seen[e].get(sid, -1) >= t[1]:
                continue
            self.seen[e][sid] = t[1]
            out.append(t)
        return out

    def _commit(self, tok, r, w):
        for k in r:
            d = self.rd.setdefault(k, {})
            d[id(tok[0])] = tok
        for k in w:
            self.lw[k] = tok
            self.rd[k] = {}
        self.alltok[id(tok[0])] = tok

    def op(self, e, fn, r=(), w=()):
        w = list(w) + [k for k in r if isinstance(k, str) and k.startswith("ps") and k[2:].isdigit()]
        r = [k for k in r if not (isinstance(k, str) and k.startswith("ps") and k[2:].isdigit())]
        waits = self._waits(e, self._deps(r, w))
        c = self.cur[e]
        if c[1] >= 30000:
            c[0], c[1] = self._newsem(e), 0
        c[1] += 1
        tok = (c[0], c[1])
        self.prog[e].append((waits, _freeze(fn), tok, 1))
        self._commit(tok, r, w)
        return tok

    def dma(self, q, out, in_, r, w, slot):
        waits = self._waits(q, self._deps(r, w))
        s = self.dsem.get(slot)
        if s is None:
            while self.free_dsems and self.free_dsems[-1][1] >= 30000:
                self.free_dsems.pop()
            s = self.dsem[slot] = self.free_dsems.pop() if self.free_dsems else [self._newsem("d"), 0]
        elif s[1] >= 30000:
            s = self.dsem[slot] = [self._newsem("d"), 0]
        s[1] += 16
        tok = (s[0], s[1])
        self.prog[q].append((waits, lambda eng: eng.dma_start(out=out, in_=in_), tok, 16))
        self._commit(tok, r, w)
        return tok

    def barrier(self):
        toks = list(self.alltok.values())
        for e in self.eng:
            waits = self._waits(e, {id(t[0]): t for t in toks})
            if waits:
                self.prog[e].append((waits, None, None, 0))
        self.lw.clear()
        self.rd.clear()
        self.free_dsems.extend(self.dsem.values())
        self.dsem = {}

    def emit(self):
        with self.nc.Block() as block:
            def run(name):
                def body(eng):
                    for waits, fn, tok, inc in self.prog[name]:
                        for s, v in waits:
                            eng.wait_ge(s, v)
                        if fn is not None:
                            fn(eng).then_inc(tok[0], inc)
                return body
            block.tensor(run('pe'))
            block.scalar(run('act'))
            block.vector(run('dve'))
            block.gpsimd(run('pool'))
            block.sync(run('sp'))


class Arena:
    def __init__(self, t, nbytes):
        self.t, self.n, self.off = t, nbytes, 0
        self.marks = []

    def push(self):
        self.marks.append(self.off)

    def pop(self):
        self.off = self.marks.pop()

    def alloc(self, free_elems, dt):
        sz = 4 if dt == F32 else 2
        nb = (free_elems * sz + 31) // 32 * 32
        assert self.off + nb <= self.n, f"arena overflow {self.off}+{nb}>{self.n}"
        v = self.t[:, self.off // 4:(self.off + nb) // 4]
        self.off += nb
        if dt != F32:
            v = v.bitcast(dt)
        return v[:, 0:free_elems]


def host_consts(NT, g):
    NS = NT * 128
    c = {}
    c['ident'] = np.eye(128, dtype=np.float32).astype(ml_dtypes.bfloat16)
    c['ones'] = np.ones((128, 128), np.float32).astype(ml_dtypes.bfloat16)
    inv = (10000.0 ** (-np.arange(0, 128, 2, dtype=np.float32) / 128)).astype(np.float32)
    pos = (np.arange(2 * NS, dtype=np.float32) + (g - 1) * NS)
    ang = (pos[:, None] * inv[None, :]).astype(np.float32)
    c['cos'] = np.cos(ang).astype(np.float32)
    c['sin'] = np.sin(ang).astype(np.float32)
    hh = np.arange(4, dtype=np.float32)
    log_g = np.log1p(-np.exp2(-5.0 - hh)).astype(np.float64)
    t = np.arange(128)
    pc = t % 64
    same = (t[:, None] // 64) == (t[None, :] // 64)
    dm = np.zeros((128, 4, 128), np.float32)
    for h in range(4):
        ex = np.abs(pc[None, :] - pc[:, None]) - pc[None, :] - 1.0
        dm[:, h, :] = np.where(same, np.exp(log_g[h] * ex), 0.0)
    c['ret_dmask'] = dm
    c['ret_qdec'] = (np.exp(log_g[None, :] * (pc[:, None] + 1.0)) * 128 ** -0.5).astype(np.float32)
    c['ret_kdec'] = np.exp(log_g[None, :] * (63.0 - pc[:, None])).astype(np.float32)
    i64 = (pc[:, None] * 0 + (t[:, None] // 64) == np.arange(2)[None, :]).astype(np.float32)
    c['ret_kdec2'] = (i64[:, :, None] * c['ret_kdec'][:, None, :]).astype(np.float32)
    c['ret_cdect'] = np.broadcast_to(np.exp(log_g * 64)[None, :], (128, 4)).astype(np.float32)
    c['ret_cdec'] = [float(np.exp(log_g[h] * 64)) for h in range(4)]
    ch = t // 32
    same32 = ch[:, None] == ch[None, :]
    ecum = (same32 & (t[:, None] <= t[None, :])).astype(np.float32)
    mid = ch * 32 + 15
    selmid = (same32 & (t[:, None] <= mid[None, :])).astype(np.float32)
    c['h_ecum'] = ecum.astype(ml_dtypes.bfloat16)
    c['h_e1'] = (ecum - selmid).astype(ml_dtypes.bfloat16)
    c['h_e2'] = (same32.astype(np.float32) - ecum).astype(ml_dtypes.bfloat16)
    c['h_mask'] = ecum.astype(np.float32)
    ind = np.zeros((128, 4), np.float32)
    ind[t, ch] = 1.0
    c['h_ind'] = ind
    c['h_indb'] = ind.astype(ml_dtypes.bfloat16)
    q = t[:, None]
    s = t[None, :]
    c['diag'] = np.where((s < 64) | (q >= 64), 0.0, -1e30).astype(np.float32)
    c['half'] = np.full((128, 1), 0.5, np.float32)
    return c


CONST_SPECS = dict(ident=([128, 128], BF16), ones=([128, 128], BF16), ret_dmask=([128, 4, 128], F32),
                   ret_qdec=([128, 4], F32), ret_kdec=([128, 4], F32), ret_kdec2=([128, 2, 4], F32), ret_cdect=([128, 4], F32), h_ecum=([128, 128], BF16),
                   h_e1=([128, 128], BF16), h_e2=([128, 128], BF16), h_mask=([128, 128], F32),
                   h_ind=([128, 4], F32), h_indb=([128, 4], BF16), diag=([128, 128], F32),
                   half=([128, 1], F32), flag=([128, 1], F32))


def build(NT, L, ncores=8, final_is_output=True, gather=True, debug_outs=()):
    NS = NT * 128
    EXT = 2 * NS
    NKT = EXT // 128
    NTG = NT // 4
    KSEL = float(min(256, EXT // 4))
    nc = bass.Bass("TRN2", target_bir_lowering=False)

    def din(name, shape, dt=F32):
        return nc.dram_tensor(name, shape, dt, kind="ExternalInput").ap()

    def dscr(name, shape, dt):
        kind = "ExternalOutput" if name in debug_outs else "Internal"
        return nc.dram_tensor(name, shape, dt, kind=kind).ap()

    xe = din("xe", [EXT, D])
    mem = din("mem", [N_MEM, D])
    norm_g = din("norm_g", [L, D])
    w_in = din("w_in", [L, D, IN_WIDTH])
    qk_g = din("qk_g", [L, 4, 128])
    ret_g = din("ret_g", [L, 4, 128])
    hgrn_g = din("hgrn_g", [L, 4, 128])
    lb_all = din("lb_logits", [2, 4, 128])
    lsel = din("lsel", [128, L])
    w_mem = din("w_mem_kv", [L, D, 2 * W])
    w_br = din("w_branch", [L, 4, W, D])
    w_out = din("w_out", [L, D, D])
    cin = {k: din("c_" + k, sh, dt) for k, (sh, dt) in CONST_SPECS.items()}
    cos_d = din("c_cos", [EXT, 64])
    sin_d = din("c_sin", [EXT, 64])
    y = nc.dram_tensor("y", [NS, D], F32, kind="ExternalOutput").ap()

    s_kT = dscr("s_kT", [4, 128, EXT], BF16)
    s_v = dscr("s_v", [EXT, 4 * VW], BF16)
    s_kiT = dscr("s_kiT", [128, EXT], BF16)
    s_qT = dscr("s_qT", [4, 128, NS], BF16)
    s_qiT = dscr("s_qiT", [4, 128, NS], BF16)
    s_w = dscr("s_w", [NS, 8], F32)
    s_sz = dscr("s_sz", [4, NS, W], BF16)
    s_bq = dscr("s_bq", [NS, W], BF16)
    s_bk = dscr("s_bk", [EXT, W], BF16)
    s_bv = dscr("s_bv", [EXT, W], BF16)
    s_logf = dscr("s_logf", [EXT, W], F32)
    s_kin = dscr("s_kin", [EXT, W], F32)
    s_ci = dscr("s_ci", [EXT, W], BF16)
    s_cq = dscr("s_cq", [NS, W], BF16)
    s_mqT = dscr("s_mqT", [4, 128, NS], BF16)
    s_sg = dscr("s_sg", [4, 16, 128, NS], BF16)
    s_brT = dscr("s_brT", [4, 4, 128, NS], BF16)
    x1 = dscr("x1", [NS, D], F32)
    x1g = dscr("x1g", [EXT, D], F32)

    es = ExitStack()
    S = Sched(nc, es)
    ARENA_BYTES = 166 * 1024
    arena_t = es.enter_context(nc.sbuf_tensor("arena", [128, ARENA_BYTES // 4], F32))
    A = Arena(arena_t, ARENA_BYTES)
    ps = [es.enter_context(nc.psum_tensor(f"ps{i}", [128, 512], F32)) for i in range(8)]
    psk = [f"ps{i}" for i in range(8)]

    def psf(i, n=512):
        return ps[i][:, 0:n]

    def psb(i):
        return ps[i][:, :].bitcast(BF16)

    C = {}
    for k, (sh, dt) in CONST_SPECS.items():
        n = int(np.prod(sh[1:]))
        C[k] = A.alloc(n, dt)
        src = cin[k] if len(sh) == 2 else cin[k].rearrange("p a b -> p (a b)")
        S.dma('sp', C[k], src, [], ["c_" + k], "c_" + k)
    ident, ones, flag = C['ident'], C['ones'], C['flag']
    dmaskT = C['ret_dmask'].rearrange("p (h t) -> p h t", h=4)
    cst = ["c_" + k for k in CONST_SPECS]
    zero_c = A.alloc(1, F32)
    S.op('dve', lambda e: e.memset(zero_c, 0.0), [], ["zero_c"])
    RET_CDEC = host_consts(1, 0)['ret_cdec']
    try:
        _ck('const')
    except _Stop:
        S.barrier(); S.emit(); return nc

    def bc_last(tab, n):
        return tab.unsqueeze(2).broadcast_to([128, tab.shape[1], n])

    def bc_mid(tab, h):
        return tab.unsqueeze(1).broadcast_to([128, h, tab.shape[1]])

    def hv(ap, h=4):
        return ap.rearrange("p (h n) -> p h n", h=h)

    rr = {}

    def rot(tag, n):
        i = rr.get(tag, 0)
        rr[tag] = i + 1
        return i % n

    try:
      for l in range(L):
          last = (l == L - 1)
          xprev = xe[0:NS, :] if l == 0 else x1g[0:NS, :]
          xown = xe[NS:EXT, :] if l == 0 else x1[:, :]
          xdst = y if (last and final_is_output) else x1
          A.push()
          gb = A.alloc(D, F32)
          S.dma('sp', gb, norm_g[l:l + 1, :].partition_broadcast(128).rearrange("p o n -> p (o n)"), [], ["gb"], "gb")
          gcols = A.alloc(4, F32)
          for i in range(4):
              S.dma('sp', gcols[:, i:i + 1], qk_g[l][i].rearrange("(p o) -> p o", o=1), [], ["gcols"], f"gcols{i}")
          gq_col = A.alloc(1, F32)
          gmq_col = A.alloc(1, F32)
          S.op('dve', lambda e: e.tensor_scalar(out=gq_col, in0=gcols[:, 0:1], scalar1=128 ** -0.5, scalar2=None, op0=ALU.mult), ["gcols"], ["gq_col"])
          S.op('dve', lambda e: e.tensor_scalar(out=gmq_col, in0=gcols[:, 2:3], scalar1=128 ** -0.5, scalar2=None, op0=ALU.mult), ["gcols"], ["gmq_col"])
          rgt = A.alloc(W, F32)
          hgt = A.alloc(W, F32)
          S.dma('sp', rgt, ret_g[l:l + 1].rearrange("o h d -> o (h d)").partition_broadcast(128).rearrange("p o n -> p (o n)"), [], ["rgt"], "rgt")
          S.dma('sp', hgt, hgrn_g[l:l + 1].rearrange("o h d -> o (h d)").partition_broadcast(128).rearrange("p o n -> p (o n)"), [], ["hgt"], "hgt")
          lb_t = A.alloc(W, F32)
          oml_t = A.alloc(W, F32)
          lsel_t = A.alloc(L, F32)
          A.push()
          la0 = A.alloc(W, F32)
          la1 = A.alloc(W, F32)
          S.dma('sp', la0, lb_all[0:1].rearrange("o h d -> o (h d)").partition_broadcast(128).rearrange("p o n -> p (o n)"), [], ["la0"], "la0")
          S.dma('sp', la1, lb_all[1:2].rearrange("o h d -> o (h d)").partition_broadcast(128).rearrange("p o n -> p (o n)"), [], ["la1"], "la1")
          S.dma('sp', lsel_t, lsel, [], ["lsel"], "lsel")
          S.op('dve', lambda e: e.tensor_tensor(out=la1, in0=la1, in1=la0, op=ALU.subtract), ["la0", "la1"], ["la1"])
          S.op('act', lambda e: e.activation(out=la0, in_=la1, func=AF.Sigmoid), ["la1"], ["la0"])
          S.op('dve', lambda e, l=l: e.tensor_scalar(out=lb_t, in0=la0, scalar1=lsel_t[:, l:l + 1], scalar2=None, op0=ALU.mult), ["la0", "lsel"], ["lb"])
          S.op('dve', lambda e: e.tensor_scalar(out=oml_t, in0=lb_t, scalar1=-1.0, scalar2=1.0, op0=ALU.mult, op1=ALU.add), ["lb"], ["oml"])
          S.barrier()
          A.pop()
          _ck('tables')

          for seg in ('prev', 'own'):
              full = seg == 'own'
              xsrc = xown if full else xprev
              eoff = NS if full else 0
              A.push()
              hT = A.alloc(16 * NS, BF16)
              hT3 = hT.rearrange("p (c t) -> p c t", c=16)
              xt = [A.alloc(D, F32) for _ in range(2)]
              hb = [A.alloc(D, BF16) for _ in range(2)]
              ss = [A.alloc(1, F32) for _ in range(2)]
              rs = [A.alloc(1, F32) for _ in range(2)]
              for t in range(NT):
                  b = t % 2
                  S.dma('sp', xt[b], xsrc[t * 128:(t + 1) * 128, :], [], [f"xt{b}"], f"xt{b}")
                  S.op('act', lambda e, b=b: e.activation(out=hb[b], in_=xt[b], func=AF.Square, accum_out=ss[b]), [f"xt{b}"], [f"hb{b}", f"ss{b}"])
                  S.op('act', lambda e, b=b: e.activation(out=rs[b], in_=ss[b], func=AF.Sqrt, scale=1.0 / D, bias=EPS), [f"ss{b}"], [f"rs{b}"])
                  S.op('dve', lambda e, b=b: e.reciprocal(out=rs[b], in_=rs[b]), [f"rs{b}"], [f"rs{b}"])
                  S.op('dve', lambda e, b=b: e.scalar_tensor_tensor(out=hb[b], in0=xt[b], scalar=rs[b], in1=gb, op0=ALU.mult, op1=ALU.mult), [f"xt{b}", f"rs{b}", "gb"], [f"hb{b}"])
                  _ck('R_a')
                  for half in range(2):
                      pi = rot("Rps", 4)
                      pv = psb(pi).rearrange("p (c t) -> p c t", c=8)
                      for c in range(8):
                          cc = half * 8 + c
                          S.op('pe', lambda e, b=b, cc=cc, c=c, pv=pv: e.transpose(out=pv[:, c, :], in_=hb[b][:, cc * 128:(cc + 1) * 128], identity=ident), [f"hb{b}", "c_ident"], [psk[pi]])
                      _ck('R_b')
                      eng = 'act' if half == 0 else 'dve'
                      dst = hT3[:, half * 8:half * 8 + 8, t * 128:(t + 1) * 128]
                      if eng == 'act':
                          S.op('act', lambda e, dst=dst, pv=pv: e.copy(out=dst, in_=pv), [psk[pi]], [f"hT{t}"])
                          _ck('R_c')
                      else:
                          S.op('dve', lambda e, dst=dst, pv=pv: e.tensor_copy(out=dst, in_=pv), [psk[pi]], [f"hT{t}"])
                          _ck('R_d')
                  _ck(f'R_t{t}')
              hTk = [f"hT{t}" for t in range(NT)]
              _ck('R_' + seg)

              wb = [A.alloc(16 * 512, BF16) for _ in range(2)]
              stg_b = [A.alloc(512, BF16) for _ in range(4)]
              stg_f = [A.alloc(512, F32) for _ in range(4)]
              tmp_f = [A.alloc(512, F32) for _ in range(4)]
              sqb = [A.alloc(512, BF16) for _ in range(2)]
              stg_v = [A.alloc(4 * VW, BF16) for _ in range(2)]
              for i in range(2):
                  S.op('pool', lambda e, i=i: e.memset(stg_v[i], 1.0), [], [f"stgv{i}"])

              def load_w(off, n, dup=False):
                  i = rot("wb", 2)
                  wv = wb[i].rearrange("p (c n) -> p c n", c=16)
                  src = w_in[l][:, off:off + n].rearrange("(c p) n -> p c n", p=128)
                  if dup:
                      S.dma('pool', wv[:, :, 0:n], src, [], [f"wb{i}"], f"wb{i}")
                      S.dma('pool', wv[:, :, n:2 * n], src, [], [f"wb{i}"], f"wb{i}b")
                  else:
                      S.dma('pool', wv[:, :, 0:n], src, [], [f"wb{i}"], f"wb{i}")
                  return wv, f"wb{i}"

              def mm_T(wv, wk, cb, tg, pi, ncol=128):
                  for c in range(16):
                      S.op('pe', lambda e, c=c: e.matmul(psf(pi), lhsT=wv[:, c, cb * 128:cb * 128 + ncol], rhs=hT3[:, c, tg * 512:(tg + 1) * 512], start=(c == 0), stop=(c == 15)),
                           [wk] + hTk[tg * 4:tg * 4 + 4], [psk[pi]])

              def mm_N(wv, wk, tt, n, pi):
                  for c in range(16):
                      S.op('pe', lambda e, c=c: e.matmul(psf(pi, n), lhsT=hT3[:, c, tt * 128:(tt + 1) * 128], rhs=wv[:, c, 0:n], start=(c == 0), stop=(c == 15)),
                           [wk, hTk[tt]], [psk[pi]])

              def evac_normT(pi, gcol, gkey, dst_fn, cb, tg):
                  j = rot("sqb", 2)
                  p2 = 4 + rot("p2", 2)
                  sb = rot("stgb", 4)
                  tf = rot("tmpf", 4)
                  S.op('act', lambda e: e.activation(out=sqb[j], in_=psf(pi), func=AF.Square), [psk[pi]], [f"sqb{j}"])
                  S.op('pe', lambda e: e.matmul(psf(p2), lhsT=ones, rhs=sqb[j], start=True, stop=True), [f"sqb{j}", "c_ones"], [psk[p2]])
                  S.op('act', lambda e: e.activation(out=tmp_f[tf], in_=psf(p2), func=AF.Sqrt, scale=1.0 / 128, bias=EPS), [psk[p2]], [f"tmpf{tf}"])
                  S.op('dve', lambda e: e.reciprocal(out=tmp_f[tf], in_=tmp_f[tf]), [f"tmpf{tf}"], [f"tmpf{tf}"])
                  S.op('dve', lambda e: e.scalar_tensor_tensor(out=stg_b[sb][:, 0:512], in0=psf(pi), scalar=gcol, in1=tmp_f[tf], op0=ALU.mult, op1=ALU.mult),
                       [psk[pi], gkey, f"tmpf{tf}"], [f"stgb{sb}"])
                  S.dma('sp', dst_fn(cb, tg), stg_b[sb][:, 0:512], [f"stgb{sb}"], [], f"stgb{sb}")

              def evac_T(pi, func, dst, npart=128):
                  sb = rot("stgb", 4)
                  if func is None:
                      S.op('dve', lambda e: e.tensor_copy(out=stg_b[sb][0:npart, 0:512], in_=ps[pi][0:npart, :]), [psk[pi]], [f"stgb{sb}"])
                  else:
                      S.op('act', lambda e: e.activation(out=stg_b[sb][0:npart, 0:512], in_=ps[pi][0:npart, :], func=func), [psk[pi]], [f"stgb{sb}"])
                  S.dma('sp', dst, stg_b[sb][0:npart, 0:512], [f"stgb{sb}"], [], f"stgb{sb}")

              tgs = range(NTG)
              tts = range(NT)

              def group_T(name, kind, **kw):
                  nblk = kw.get('nblk', 4)
                  for blk0 in range(0, nblk, 4):
                      nb_here = min(4, nblk - blk0)
                      dup = kw.get('dup', False)
                      wv, wk = load_w(OFF[name] + kw.get('coloff', 0) + blk0 * 128, kw.get('ncols', nb_here * 128), dup)
                      for cb in range(nb_here):
                          for tg in tgs:
                              pi = rot("Ips", 4)
                              mm_T(wv, wk, cb, tg, pi)
                              kw['evac'](pi, blk0 + cb, tg)

              if full:
                  group_T('a_q', 'normT', evac=lambda pi, cb, tg: evac_normT(pi, gq_col, "gq_col", lambda cb, tg: s_qT[cb][:, tg * 512:(tg + 1) * 512], cb, tg))
              group_T('a_k', 'normT', evac=lambda pi, cb, tg: evac_normT(pi, gcols[:, 1:2], "gcols", lambda cb, tg: s_kT[cb][:, eoff + tg * 512:eoff + (tg + 1) * 512], cb, tg))
              if full:
                  group_T('a_qi', 'plainT', evac=lambda pi, cb, tg: evac_T(pi, None, s_qiT[cb][:, tg * 512:(tg + 1) * 512]))
                  group_T('m_q', 'normT', evac=lambda pi, cb, tg: evac_normT(pi, gmq_col, "gmq_col", lambda cb, tg: s_mqT[cb][:, tg * 512:(tg + 1) * 512], cb, tg))
              group_T('a_ki', 'plainT', nblk=1, ncols=64, dup=True, evac=lambda pi, cb, tg: evac_T(pi, None, s_kiT[:, eoff + tg * 512:eoff + (tg + 1) * 512]))
              if full:
                  for n in range(4):
                      for c4 in range(4):
                          group_T('gates', 'sigT', coloff=n * D + c4 * 512, nblk=4,
                                  evac=lambda pi, cb, tg, n=n, c4=c4: evac_T(pi, AF.Sigmoid, s_sg[n][c4 * 4 + cb][:, tg * 512:(tg + 1) * 512]))

              def group_N(name, evac, n=512):
                  wv, wk = load_w(OFF[name], n)
                  for tt in tts:
                      pi = rot("Ips", 4)
                      mm_N(wv, wk, tt, n, pi)
                      evac(pi, tt)

              def evac_N_plain(pi, tt, dst, func=None, eng='dve'):
                  sb = rot("stgb", 4)
                  if func is None and eng == 'dve':
                      S.op('dve', lambda e: e.tensor_copy(out=stg_b[sb][:, 0:512], in_=psf(pi)), [psk[pi]], [f"stgb{sb}"])
                  elif func is None:
                      S.op('act', lambda e: e.copy(out=stg_b[sb][:, 0:512], in_=psf(pi)), [psk[pi]], [f"stgb{sb}"])
                  else:
                      S.op('act', lambda e: e.activation(out=stg_b[sb][:, 0:512], in_=psf(pi), func=func), [psk[pi]], [f"stgb{sb}"])
                  S.dma('sp', dst[tt * 128:(tt + 1) * 128, :], stg_b[sb][:, 0:512], [f"stgb{sb}"], [], f"stgb{sb}")

              def evac_V(pi, tt):
                  sb = rot("stgv", 2)
                  dv = stg_v[sb].rearrange("p (h n) -> p h n", h=4)[:, :, 0:128]
                  S.op('act', lambda e: e.copy(out=dv, in_=hv(psf(pi))), [psk[pi]], [f"stgv{sb}"])
                  S.dma('sp', s_v[eoff + tt * 128:eoff + (tt + 1) * 128, :], stg_v[sb], [f"stgv{sb}"], [], f"stgv{sb}")

              cs_t = [(A.alloc(64, F32), A.alloc(64, F32)) for _ in range(2)]

              def evac_rope(pi, tt, dst):
                  j = rot("cs", 2)
                  ct, st = cs_t[j]
                  S.dma('sp', ct, cos_d[eoff + tt * 128:eoff + (tt + 1) * 128, :], [], [f"cs{j}"], f"cs{j}a")
                  S.dma('sp', st, sin_d[eoff + tt * 128:eoff + (tt + 1) * 128, :], [], [f"cs{j}"], f"cs{j}b")
                  pv = psf(pi).rearrange("p (h two d) -> p h two d", h=4, two=2)
                  t1, t2 = pv[:, :, 0, :], pv[:, :, 1, :]
                  cb_, sb_ = bc_mid(ct, 4), bc_mid(st, 4)
                  fa, fb, fc, fd = [tmp_f[i][:, 0:256].rearrange("p (h d) -> p h d", h=4) for i in range(4)]
                  ks = [f"tmpf{i}" for i in range(4)]
                  S.op('dve', lambda e: e.tensor_tensor(out=fa, in0=t1, in1=cb_, op=ALU.mult), [psk[pi], f"cs{j}"], [ks[0]])
                  S.op('dve', lambda e: e.tensor_tensor(out=fb, in0=t2, in1=sb_, op=ALU.mult), [psk[pi], f"cs{j}"], [ks[1]])
                  S.op('dve', lambda e: e.tensor_tensor(out=fc, in0=t1, in1=sb_, op=ALU.mult), [psk[pi], f"cs{j}"], [ks[2]])
                  S.op('dve', lambda e: e.tensor_tensor(out=fd, in0=t2, in1=cb_, op=ALU.mult), [psk[pi], f"cs{j}"], [ks[3]])
                  sbi = rot("stgb", 4)
                  ov = stg_b[sbi][:, 0:512].rearrange("p (h two d) -> p h two d", h=4, two=2)
                  S.op('pool', lambda e: e.tensor_tensor(out=ov[:, :, 0, :], in0=fa, in1=fb, op=ALU.subtract), [ks[0], ks[1]], [f"stgb{sbi}"])
                  S.op('pool', lambda e: e.tensor_tensor(out=ov[:, :, 1, :], in0=fc, in1=fd, op=ALU.add), [ks[2], ks[3]], [f"stgb{sbi}"])
                  S.dma('sp', dst[tt * 128:(tt + 1) * 128, :], stg_b[sbi][:, 0:512], [f"stgb{sbi}"], [], f"stgb{sbi}")

              def evac_f(pi, tt):
                  i0 = rot("stgf", 4)
                  i1 = rot("stgf", 4)
                  S.op('act', lambda e: e.activation(out=stg_f[i0], in_=psf(pi), func=AF.Sigmoid), [psk[pi]], [f"stgf{i0}"])
                  S.op('dve', lambda e: e.tensor_tensor(out=stg_f[i0], in0=stg_f[i0], in1=oml_t, op=ALU.mult), [f"stgf{i0}", "oml"], [f"stgf{i0}"])
                  S.op('dve', lambda e: e.tensor_tensor(out=stg_f[i0], in0=stg_f[i0], in1=lb_t, op=ALU.add), [f"stgf{i0}", "lb"], [f"stgf{i0}"])
                  S.op('dve', lambda e: e.tensor_scalar(out=stg_f[i1], in0=stg_f[i0], scalar1=-1.0, scalar2=1.0, op0=ALU.mult, op1=ALU.add), [f"stgf{i0}"], [f"stgf{i1}"])
                  S.dma('sp', s_kin[eoff + tt * 128:eoff + (tt + 1) * 128, :], stg_f[i1], [f"stgf{i1}"], [], f"stgf{i1}")
                  S.op('act', lambda e: e.activation(out=stg_f[i0], in_=stg_f[i0], func=AF.Ln), [f"stgf{i0}"], [f"stgf{i0}"])
                  S.dma('sp', s_logf[eoff + tt * 128:eoff + (tt + 1) * 128, :], stg_f[i0], [f"stgf{i0}"], [], f"stgf{i0}")

              def evac_w(pi, tt):
                  i0 = rot("stgf", 4)
                  S.op('dve', lambda e: e.tensor_copy(out=stg_f[i0][:, 0:8], in_=psf(pi, 8)), [psk[pi]], [f"stgf{i0}"])
                  S.dma('sp', s_w[tt * 128:(tt + 1) * 128, :], stg_f[i0][:, 0:8], [f"stgf{i0}"], [], f"stgf{i0}")

              group_N('a_v', evac_V)
              group_N('b_k', lambda pi, tt: evac_rope(pi, tt, s_bk[eoff:eoff + NS, :]))
              group_N('b_v', lambda pi, tt: evac_N_plain(pi, tt, s_bv[eoff:eoff + NS, :], eng='act'))
              group_N('c_f', evac_f)
              group_N('c_i', lambda pi, tt: evac_N_plain(pi, tt, s_ci[eoff:eoff + NS, :]))
              if full:
                  group_N('a_wi', evac_w, n=8)
                  group_N('b_q', lambda pi, tt: evac_rope(pi, tt, s_bq))
                  group_N('c_q', lambda pi, tt: evac_N_plain(pi, tt, s_cq, eng='act'))
                  for bi, nm in enumerate(('a_z', 'b_z', 'c_z', 'm_z')):
                      group_N(nm, lambda pi, tt, bi=bi: evac_N_plain(pi, tt, s_sz[bi], func=AF.Silu))
              S.barrier()
              A.pop()
              _ck('I_' + seg)

          def epilogue(n, po, t, sz_t, szk, gtab, gkey, norm, bufs, eps_bank, dens=None):
              br, brk, brT, brTk, sc4, sck, gsz = bufs
              pviews, pkeys = po
              if norm:
                  ssq, rsq = sc4
                  for h in range(4):
                      S.op('act', lambda e, h=h: e.activation(out=br[:, h * 128:(h + 1) * 128], in_=pviews[h], func=AF.Square, accum_out=ssq[:, h:h + 1]), pkeys, [brk, sck + "s"])
                  S.op('act', lambda e: e.activation(out=rsq, in_=ssq, func=AF.Sqrt, scale=1.0 / 128, bias=EPS), [sck + "s"], [sck + "r"])
                  S.op('dve', lambda e: e.reciprocal(out=rsq, in_=rsq), [sck + "r"], [sck + "r"])
                  S.op('pool', lambda e: e.tensor_tensor(out=gsz, in0=sz_t, in1=gtab, op=ALU.mult), [szk, gkey], [sck + "g"])
                  for h in range(4):
                      S.op('dve', lambda e, h=h: e.scalar_tensor_tensor(out=br[:, h * 128:(h + 1) * 128], in0=pviews[h], scalar=rsq[:, h:h + 1], in1=gsz[:, h * 128:(h + 1) * 128], op0=ALU.mult, op1=ALU.mult),
                           pkeys + [sck + "r", sck + "g"], [brk])
              else:
                  rden = sc4[1]
                  for h in range(4):
                      S.op('dve', lambda e, h=h: e.reciprocal(out=rden[:, h:h + 1], in_=dens[h]), pkeys, [sck + "r"])
                  for h in range(4):
                      S.op('dve', lambda e, h=h: e.scalar_tensor_tensor(out=br[:, h * 128:(h + 1) * 128], in0=pviews[h], scalar=rden[:, h:h + 1], in1=sz_t[:, h * 128:(h + 1) * 128], op0=ALU.mult, op1=ALU.mult),
                           pkeys + [sck + "r", szk], [brk])
              pt_i = eps_bank
              pv = psb(pt_i)[:, 0:512].rearrange("p (k t) -> p k t", k=4)
              for k in range(4):
                  S.op('pe', lambda e, k=k: e.transpose(out=pv[:, k, :], in_=br[:, k * 128:(k + 1) * 128], identity=ident), [brk, "c_ident"], [psk[pt_i]])
              S.op('act', lambda e: e.copy(out=brT.rearrange("p (k t) -> p k t", k=4), in_=pv), [psk[pt_i]], [brTk])
              S.dma('sp', s_brT[n][:, :, t * 128:(t + 1) * 128].rearrange("k p t -> p k t"), brT.rearrange("p (k t) -> p k t", k=4), [brTk], [], brTk)

          def ep_bufs(tag):
              out = []
              for i in range(2):
                  out.append((A.alloc(W, BF16), f"{tag}br{i}", A.alloc(W, BF16), f"{tag}brT{i}", (A.alloc(4, F32), A.alloc(4, F32)), f"{tag}sc{i}", A.alloc(W, F32)))
              return out

          A.push()
          kT_sb = A.alloc(4 * EXT, BF16)
          kT3 = kT_sb.rearrange("p (h s) -> p h s", h=4)
          V1 = A.alloc(NKT * 4 * VW, BF16)
          V4 = V1.rearrange("p (k h n) -> p k h n", k=NKT, h=4)
          kiT = A.alloc(EXT, BF16)
          for h in range(4):
              S.dma('sp', kT3[:, h, :], s_kT[h], [], ["kT_sb"], f"kTsb{h}")
          S.dma('sp', V1.rearrange("p (k n) -> p k n", k=NKT), s_v.rearrange("(k p) n -> p k n", p=128), [], ["V1"], "V1")
          S.dma('sp', kiT, s_kiT, [], ["kiT"], "kiT")
          score = A.alloc(EXT, F32)
          junk = [A.alloc(NS, BF16)] * 2
          mk = [A.alloc(EXT, BF16) for _ in range(2)]
          maskT = A.alloc(EXT, BF16)
          maskT3 = maskT.rearrange("p (k q) -> p k q", k=NKT)
          rl = [A.alloc(512, BF16) for _ in range(3)]
          ex = [A.alloc(512, BF16) for _ in range(2)]
          ptb = [A.alloc(512, BF16) for _ in range(2)]
          qn = [A.alloc(512, BF16) for _ in range(2)]
          qi = [A.alloc(512, BF16) for _ in range(2)]
          wt = [A.alloc(8, F32) for _ in range(2)]
          sz_a = [A.alloc(W, BF16) for _ in range(2)]
          sm = [[A.alloc(1, F32) for _ in range(8)] for _ in range(2)]
          epb = ep_bufs("A")

          def a_stage1(j):
              b = j % 2
              nk = NT + j + 1
              Sk = nk * 128
              S.dma('sp', qi[b].rearrange("p (h t) -> p h t", h=4), s_qiT[:, :, j * 128:(j + 1) * 128].rearrange("h p t -> p h t"), [], [f"qi{b}"], f"qi{b}")
              S.dma('sp', wt[b], s_w[j * 128:(j + 1) * 128, :], [], [f"wt{b}"], f"wt{b}")
              qi3 = qi[b].rearrange("p (h t) -> p h t", h=4)
              nsg = (Sk + 511) // 512
              for sg in range(nsg):
                  n = min(512, Sk - sg * 512)
                  for h in range(8):
                      pi = rot("Aips", 3)
                      ri = rot("rl", 3)
                      base = 64 * (h % 2)
                      S.op('pe', lambda e, h=h, base=base, n=n, sg=sg, pi=pi: e.matmul(psf(pi, n), lhsT=qi3[base:base + 64, h // 2, :], rhs=kiT[base:base + 64, sg * 512:sg * 512 + n], start=True, stop=True),
                           [f"qi{b}", "kiT"], [psk[pi]])
                      S.op('act', lambda e, n=n, pi=pi, ri=ri: e.activation(out=rl[ri][:, 0:n], in_=psf(pi, n), func=AF.Relu), [psk[pi]], [f"rl{ri}"])
                      sc = score[:, sg * 512:sg * 512 + n]
                      if h == 0:
                          S.op('dve', lambda e, n=n, ri=ri, sc=sc: e.tensor_scalar(out=sc, in0=rl[ri][:, 0:n], scalar1=wt[b][:, 0:1], scalar2=None, op0=ALU.mult), [f"rl{ri}", f"wt{b}"], [f"score{sg}"])
                      else:
                          S.op('dve', lambda e, n=n, ri=ri, sc=sc, h=h: e.scalar_tensor_tensor(out=sc, in0=rl[ri][:, 0:n], scalar=wt[b][:, h:h + 1], in1=sc, op0=ALU.mult, op1=ALU.add), [f"rl{ri}", f"wt{b}", f"score{sg}"], [f"score{sg}"])
              sck = [f"score{sg}" for sg in range(nsg)]
              amax, w0, lo, step, mid, cp, co, cnt = sm[b]
              smk = f"sm{b}"
              S.op('dve', lambda e: e.tensor_reduce(out=amax, in_=score[:, 0:Sk], axis=AX.X, op=ALU.max, apply_absolute_value=True), sck, [smk])
              S.op('dve', lambda e: e.tensor_tensor(out=score[:, Sk - 128:Sk], in0=score[:, Sk - 128:Sk], in1=C['diag'], op=ALU.add), sck + ["c_diag"], sck)
              S.op('dve', lambda e: e.tensor_scalar(out=lo, in0=amax, scalar1=-1.0, scalar2=-1.0, op0=ALU.mult, op1=ALU.add), [smk], [smk + "lo"])
              S.op('dve', lambda e: e.tensor_scalar(out=w0, in0=amax, scalar1=2.0, scalar2=2.0, op0=ALU.mult, op1=ALU.add), [smk], [smk + "w0"])
              for it in range(NITER):
                  f = 2.0 ** (-(it + 1))
                  S.op('dve', lambda e, f=f: e.tensor_scalar(out=step, in0=w0, scalar1=f, scalar2=None, op0=ALU.mult), [smk + "w0"], [smk + "st"])
                  S.op('dve', lambda e: e.tensor_tensor(out=mid, in0=lo, in1=step, op=ALU.add), [smk + "lo", smk + "st"], [smk + "mid"])
                  S.op('dve', lambda e: e.tensor_scalar(out=junk[0], in0=score[:, 0:NS], scalar1=mid, scalar2=None, op0=ALU.is_ge, op1=ALU.add, accum_out=cp), sck + [smk + "mid"], ["junk", smk + "cp"])
                  S.op('dve', lambda e: e.tensor_scalar(out=junk[1][:, 0:Sk - NS], in0=score[:, NS:Sk], scalar1=mid, scalar2=None, op0=ALU.is_ge, op1=ALU.add, accum_out=co), sck + [smk + "mid"], ["junk", smk + "co"])
                  S.op('dve', lambda e: e.scalar_tensor_tensor(out=cnt, in0=cp, scalar=flag, in1=co, op0=ALU.mult, op1=ALU.add), [smk + "cp", smk + "co", "c_flag"], [smk + "cnt"])
                  S.op('dve', lambda e: e.tensor_scalar(out=cnt, in0=cnt, scalar1=KSEL - 0.5, scalar2=step, op0=ALU.is_ge, op1=ALU.mult), [smk + "cnt", smk + "st"], [smk + "cnt"])
                  S.op('dve', lambda e: e.tensor_tensor(out=lo, in0=lo, in1=cnt, op=ALU.add), [smk + "lo", smk + "cnt"], [smk + "lo"])
              S.op('dve', lambda e: e.tensor_scalar(out=mk[b][:, 0:NS], in0=score[:, 0:NS], scalar1=lo, scalar2=flag, op0=ALU.is_ge, op1=ALU.mult), sck + [smk + "lo", "c_flag"], [f"mk{b}"])
              S.op('dve', lambda e: e.tensor_scalar(out=mk[b][:, NS:Sk], in0=score[:, NS:Sk], scalar1=lo, scalar2=None, op0=ALU.is_ge), sck + [smk + "lo"], [f"mk{b}"])

          def a_stage2(j):
              b = j % 2
              nk = NT + j + 1
              S.dma('sp', qn[b].rearrange("p (h t) -> p h t", h=4), s_qT[:, :, j * 128:(j + 1) * 128].rearrange("h p t -> p h t"), [], [f"qn{b}"], f"qn{b}")
              S.dma('sp', sz_a[b], s_sz[0][j * 128:(j + 1) * 128, :], [], [f"sza{b}"], f"sza{b}")
              qn3 = qn[b].rearrange("p (h t) -> p h t", h=4)
              for k0 in range(0, nk, 8):
                  kn = min(8, nk - k0)
                  pv = psb(3).rearrange("p (k t) -> p k t", k=8)
                  for kk in range(kn):
                      kt = k0 + kk
                      S.op('pe', lambda e, kk=kk, kt=kt: e.transpose(out=pv[:, kk, :], in_=mk[b][:, kt * 128:(kt + 1) * 128], identity=ident), [f"mk{b}", "c_ident"], [psk[3]])
                  S.op('act', lambda e, k0=k0, kn=kn: e.copy(out=maskT3[:, k0:k0 + kn, :], in_=pv[:, 0:kn, :]), [psk[3]], ["maskT"])
              o_views = []
              for h in range(4):
                  o_views.append(ps[6 + h // 2][:, (h % 2) * VW:(h % 2) * VW + VW])
              for kt in range(nk):
                  pl = 4 + rot("Apl", 2)
                  plv = hv(psf(pl))
                  for h in range(4):
                      S.op('pe', lambda e, h=h, kt=kt, plv=plv: e.matmul(plv[:, h, :], lhsT=kT3[:, h, kt * 128:(kt + 1) * 128], rhs=qn3[:, h, :], start=True, stop=True, skip_group_check=True), ["kT_sb", f"qn{b}"], [psk[pl]])
                  xi = rot("Aex", 2)
                  S.op('act', lambda e, xi=xi, pl=pl: e.activation(out=ex[xi], in_=psf(pl), func=AF.Exp), [psk[pl]], [f"ex{xi}"])
                  pi_ = rot("Apt", 2)
                  S.op('pool', lambda e, xi=xi, pi_=pi_, kt=kt: e.tensor_tensor(out=hv(ptb[pi_]), in0=hv(ex[xi]), in1=bc_mid(maskT3[:, kt, :], 4), op=ALU.mult), [f"ex{xi}", "maskT"], [f"pt{pi_}"])
                  for h in range(4):
                      S.op('pe', lambda e, h=h, kt=kt, pi_=pi_: e.matmul(o_views[h][:, 0:129], lhsT=hv(ptb[pi_])[:, h, :], rhs=V4[:, kt, h, 0:129], start=(kt == 0 and h % 2 == 0), stop=(kt == nk - 1), skip_group_check=True),
                           [f"pt{pi_}", "V1"], [psk[6 + h // 2]])
              epilogue(0, ([o_views[h][:, 0:128] for h in range(4)], [psk[6], psk[7]]), j, sz_a[b], f"sza{b}", None, None, False, epb[b], 3, [o_views[h][:, 128:129] for h in range(4)])

          a_stage1(0)
          for j in range(NT):
              if j + 1 < NT:
                  a_stage1(j + 1)
              a_stage2(j)
          S.barrier()
          A.pop()

          _ck('A')
          A.push()
          memT = A.alloc(16 * N_MEM, BF16)
          memT3 = memT.rearrange("p (c m) -> p c m", c=16)
          mf = [A.alloc(D, F32) for _ in range(2)]
          mbf = [A.alloc(D, BF16) for _ in range(2)]
          for mt in range(2):
              S.dma('sp', mf[mt], mem[mt * 128:(mt + 1) * 128, :], [], [f"mf{mt}"], f"mf{mt}")
              S.op('dve', lambda e, mt=mt: e.tensor_copy(out=mbf[mt], in_=mf[mt]), [f"mf{mt}"], [f"mbf{mt}"])
              for half in range(2):
                  pi = rot("Mps", 2)
                  pv = psb(pi).rearrange("p (c t) -> p c t", c=8)
                  for c in range(8):
                      cc = half * 8 + c
                      S.op('pe', lambda e, mt=mt, cc=cc, c=c, pv=pv: e.transpose(out=pv[:, c, :], in_=mbf[mt][:, cc * 128:(cc + 1) * 128], identity=ident), [f"mbf{mt}", "c_ident"], [psk[pi]])
                  S.op('act', lambda e, half=half, mt=mt, pv=pv: e.copy(out=memT3[:, half * 8:half * 8 + 8, mt * 128:(mt + 1) * 128], in_=pv), [psk[pi]], ["memT"])
          wk_ = A.alloc(16 * 512, BF16)
          wv_ = A.alloc(16 * 512, BF16)
          wk3 = wk_.rearrange("p (c n) -> p c n", c=16)
          wv3 = wv_.rearrange("p (c n) -> p c n", c=16)
          S.dma('pool', wk3, w_mem[l][:, 0:W].rearrange("(c p) n -> p c n", p=128), [], ["wk_"], "wk_")
          S.dma('pool', wv3, w_mem[l][:, W:2 * W].rearrange("(c p) n -> p c n", p=128), [], ["wv_"], "wv_")
          mkT = A.alloc(4 * N_MEM, BF16)
          mkT3 = mkT.rearrange("p (h m) -> p h m", h=4)
          V1m = A.alloc(2 * 4 * VW, BF16)
          V1m4 = V1m.rearrange("p (k h n) -> p k h n", k=2, h=4)
          S.op('pool', lambda e: e.memset(V1m, 1.0), [], ["V1m"])
          msq = A.alloc(N_MEM, BF16)
          mtmp = A.alloc(N_MEM, F32)
          for h in range(4):
              pi = rot("Mps", 2)
              for c in range(16):
                  S.op('pe', lambda e, c=c, h=h, pi=pi: e.matmul(psf(pi, N_MEM), lhsT=wk3[:, c, h * 128:(h + 1) * 128], rhs=memT3[:, c, :], start=(c == 0), stop=(c == 15)), ["wk_", "memT"], [psk[pi]])
              S.op('act', lambda e, pi=pi: e.activation(out=msq, in_=psf(pi, N_MEM), func=AF.Square), [psk[pi]], ["msq"])
              S.op('pe', lambda e: e.matmul(psf(2, N_MEM), lhsT=ones, rhs=msq, start=True, stop=True), ["msq", "c_ones"], [psk[2]])
              S.op('act', lambda e: e.activation(out=mtmp, in_=psf(2, N_MEM), func=AF.Sqrt, scale=1.0 / 128, bias=EPS), [psk[2]], ["mtmp"])
              S.op('dve', lambda e: e.reciprocal(out=mtmp, in_=mtmp), ["mtmp"], ["mtmp"])
              S.op('dve', lambda e, h=h, pi=pi: e.scalar_tensor_tensor(out=mkT3[:, h, :], in0=psf(pi, N_MEM), scalar=gcols[:, 3:4], in1=mtmp, op0=ALU.mult, op1=ALU.mult), [psk[pi], "gcols", "mtmp"], ["mkT"])
          for mt in range(2):
              pi = rot("Mps", 2)
              for c in range(16):
                  S.op('pe', lambda e, c=c, mt=mt, pi=pi: e.matmul(psf(pi), lhsT=memT3[:, c, mt * 128:(mt + 1) * 128], rhs=wv3[:, c, :], start=(c == 0), stop=(c == 15)), ["wv_", "memT"], [psk[pi]])
              S.op('act', lambda e, mt=mt, pi=pi: e.copy(out=V1m4[:, mt, :, 0:128], in_=hv(psf(pi))), [psk[pi]], ["V1m"])
          mq = [A.alloc(512, BF16) for _ in range(2)]
          sz_m = [A.alloc(W, BF16) for _ in range(2)]
          ptm = [A.alloc(512, BF16) for _ in range(2)]
          epb = ep_bufs("M")
          for j in range(NT):
              b = j % 2
              S.dma('sp', mq[b].rearrange("p (h t) -> p h t", h=4), s_mqT[:, :, j * 128:(j + 1) * 128].rearrange("h p t -> p h t"), [], [f"mq{b}"], f"mq{b}")
              S.dma('sp', sz_m[b], s_sz[3][j * 128:(j + 1) * 128, :], [], [f"szm{b}"], f"szm{b}")
              mq3 = mq[b].rearrange("p (h t) -> p h t", h=4)
              o_views = [ps[6 + h // 2][:, (h % 2) * VW:(h % 2) * VW + VW] for h in range(4)]
              for mt in range(2):
                  pl = 4 + rot("Mpl", 2)
                  plv = hv(psf(pl))
                  for h in range(4):
                      S.op('pe', lambda e, h=h, mt=mt, plv=plv, mq3=mq3: e.matmul(plv[:, h, :], lhsT=mkT3[:, h, mt * 128:(mt + 1) * 128], rhs=mq3[:, h, :], start=True, stop=True, skip_group_check=True), ["mkT", f"mq{b}"], [psk[pl]])
                  pi_ = rot("Mpt", 2)
                  S.op('act', lambda e, pi_=pi_, pl=pl: e.activation(out=ptm[pi_], in_=psf(pl), func=AF.Exp), [psk[pl]], [f"ptm{pi_}"])
                  for h in range(4):
                      S.op('pe', lambda e, h=h, mt=mt, pi_=pi_: e.matmul(o_views[h][:, 0:129], lhsT=hv(ptm[pi_])[:, h, :], rhs=V1m4[:, mt, h, 0:129], start=(mt == 0 and h % 2 == 0), stop=(mt == 1), skip_group_check=True),
                           [f"ptm{pi_}", "V1m"], [psk[6 + h // 2]])
              epilogue(3, ([o_views[h][:, 0:128] for h in range(4)], [psk[6], psk[7]]), j, sz_m[b], f"szm{b}", None, None, False, epb[b], 3, [o_views[h][:, 128:129] for h in range(4)])
          S.barrier()
          A.pop()

          _ck('M')
          A.push()
          Sst = A.alloc(W, F32)
          Sst3 = hv(Sst)
          Sbf = [A.alloc(W, BF16) for _ in range(2)]
          S.op('pool', lambda e: e.memset(Sst, 0.0), [], ["Sst"])
          S.op('pool', lambda e: e.memset(Sbf[0], 0.0), [], ["Sbf0"])
          bqt = [A.alloc(W, BF16) for _ in range(2)]
          bkt = [A.alloc(W, BF16) for _ in range(2)]
          bvt = [A.alloc(W, BF16) for _ in range(2)]
          kd = [A.alloc(W, BF16) for _ in range(2)]
          qd = [A.alloc(W, BF16) for _ in range(2)]
          qdT = [A.alloc(W, BF16) for _ in range(2)]
          kTt = [A.alloc(W, BF16) for _ in range(2)]
          sTm = [A.alloc(W, BF16) for _ in range(2)]
          qdTm = [A.alloc(W, BF16) for _ in range(2)]
          for c in range(2):
              S.op('pool', lambda e, c=c: e.memset(qdTm[c], 0.0), [], [f"qdTm{c}"])
          sz_b = [A.alloc(W, BF16) for _ in range(2)]
          epb = ep_bufs("B")
          sbi = 0
          for seg in ('prev', 'own'):
              full = seg == 'own'
              eoff = NS if full else 0
              for t in range(NT):
                  b = rot("Bb", 2)
                  S.dma('sp', bkt[b], s_bk[eoff + t * 128:eoff + (t + 1) * 128, :], [], [f"bkt{b}"], f"bkt{b}")
                  S.dma('sp', bvt[b], s_bv[eoff + t * 128:eoff + (t + 1) * 128, :], [], [f"bvt{b}"], f"bvt{b}")
                  kdec2 = C['ret_kdec2'].rearrange("p (c h) -> p c h", c=2)
                  for c in range(2):
                      S.op('pool', lambda e, b=b, c=c: e.tensor_tensor(out=hv(kd[c]), in0=hv(bkt[b]), in1=bc_last(kdec2[:, c, :], 128), op=ALU.mult), [f"bkt{b}", "c_ret_kdec2"], [f"kd{c}"])
                  _ck('B_a')
                  if full:
                      S.dma('sp', bqt[b], s_bq[t * 128:(t + 1) * 128, :], [], [f"bqt{b}"], f"bqt{b}")
                      S.dma('sp', sz_b[b], s_sz[1][t * 128:(t + 1) * 128, :], [], [f"szb{b}"], f"szb{b}")
                      S.op('pool', lambda e, b=b: e.tensor_tensor(out=hv(qd[b]), in0=hv(bqt[b]), in1=bc_last(C['ret_qdec'], 128), op=ALU.mult), [f"bqt{b}", "c_ret_qdec"], [f"qd{b}"])
                      _ck('B_b1')
                      pq = psb(0)[:, 0:512].rearrange("p (h t) -> p h t", h=4)
                      pk = psb(1)[:, 0:512].rearrange("p (h t) -> p h t", h=4)
                      for h in range(4):
                          S.op('pe', lambda e, h=h, b=b: e.transpose(out=pq[:, h, :], in_=qd[b][:, h * 128:(h + 1) * 128], identity=ident), [f"qd{b}", "c_ident"], [psk[0]])
                      for h in range(4):
                          S.op('pe', lambda e, h=h, b=b: e.transpose(out=pk[:, h, :], in_=bkt[b][:, h * 128:(h + 1) * 128], identity=ident), [f"bkt{b}", "c_ident"], [psk[1]])
                      S.op('act', lambda e, b=b: e.copy(out=hv(qdT[b]), in_=pq), [psk[0]], [f"qdT{b}"])
                      for c in range(2):
                          S.op('dve', lambda e, c=c: e.tensor_copy(out=hv(qdTm[c])[:, :, 64 * c:64 * c + 64], in_=pq[:, :, 64 * c:64 * c + 64]), [psk[0]], [f"qdTm{c}"])
                      S.op('act', lambda e, b=b: e.copy(out=hv(kTt[b]), in_=pk), [psk[1]], [f"kTt{b}"])
                      _ck('B_b3')
                      pS = hv(psf(2))
                      for h in range(4):
                          S.op('pe', lambda e, h=h, b=b: e.matmul(pS[:, h, :], lhsT=hv(kTt[b])[:, h, :], rhs=hv(qdT[b])[:, h, :], start=True, stop=True, skip_group_check=True), [f"kTt{b}", f"qdT{b}"], [psk[2]])
                      S.op('dve', lambda e, b=b: e.tensor_tensor(out=hv(sTm[b]), in0=pS, in1=dmaskT, op=ALU.mult), [psk[2], "c_ret_dmask"], [f"sTm{b}"])
                      _ck('B_c')
                      po_i = 3
                      pO = hv(psf(po_i))
                      for h in range(4):
                          S.op('pe', lambda e, h=h, b=b: e.matmul(pO[:, h, :], lhsT=hv(sTm[b])[:, h, :], rhs=bvt[b][:, h * 128:(h + 1) * 128], start=(h == 0), stop=False, skip_group_check=True), [f"sTm{b}", f"bvt{b}"], [psk[po_i]])
                  for c in range(2):
                      if full:
                          for h in range(4):
                              S.op('pe', lambda e, h=h, b=b, c=c, sbi=sbi: e.matmul(pO[:, h, :], lhsT=hv(qdTm[c])[:, h, :], rhs=hv(Sbf[sbi])[:, h, :], start=False, stop=(c == 1), skip_group_check=True),
                                   [f"qdTm{c}", f"Sbf{sbi}"], [psk[po_i]])
                      pkv_i = 5 + rot("Bkv", 1)
                      pKV = hv(psf(pkv_i))
                      for h in range(4):
                          S.op('pe', lambda e, h=h, b=b, c=c: e.matmul(pKV[:, h, :], lhsT=kd[c][:, h * 128:(h + 1) * 128], rhs=bvt[b][:, h * 128:(h + 1) * 128], start=True, stop=True, skip_group_check=True),
                               [f"kd{c}", f"bvt{b}"], [psk[pkv_i]])
                      _ck('B_f' + str(c))
                      S.op('dve', lambda e: e.tensor_tensor(out=Sst3, in0=Sst3, in1=bc_last(C['ret_cdect'], 128), op=ALU.mult), ["Sst", "c_ret_cdect"], ["Sst"])
                      S.op('dve', lambda e, pKV=pKV: e.tensor_tensor(out=Sst3, in0=pKV, in1=Sst3, op=ALU.add), ["Sst", psk[pkv_i]], ["Sst"])
                      if (not full) and t == NT - 1 and c == 1:
                          S.op('dve', lambda e: e.tensor_scalar(out=Sst, in0=Sst, scalar1=flag, scalar2=None, op0=ALU.mult), ["Sst", "c_flag"], ["Sst"])
                      sbi = 1 - sbi
                      S.op('act', lambda e, sbi=sbi: e.copy(out=Sbf[sbi], in_=Sst), ["Sst"], [f"Sbf{sbi}"])
                      _ck('B_g' + str(c))
                  if full:
                      epilogue(1, ([pO[:, h, :] for h in range(4)], [psk[po_i]]), t, sz_b[b], f"szb{b}", rgt, "rgt", True, epb[b], 4)
                      _ck('B_h')
          S.barrier()
          A.pop()

          _ck('B')
          A.push()
          Sst = A.alloc(W, F32)
          Sst3 = hv(Sst)
          Sbf = [A.alloc(W, BF16) for _ in range(2)]
          S.op('pool', lambda e: e.memset(Sst, 0.0), [], ["Sst"])
          S.op('pool', lambda e: e.memset(Sbf[0], 0.0), [], ["Sbf0"])
          lf = [A.alloc(W, F32) for _ in range(2)]
          kin = [A.alloc(W, F32) for _ in range(2)]
          ci = [A.alloc(W, BF16) for _ in range(2)]
          cq = [A.alloc(W, BF16) for _ in range(2)]
          sz_c = [A.alloc(W, BF16) for _ in range(2)]
          lhi = [A.alloc(W, BF16) for _ in range(2)]
          llo = [A.alloc(W, BF16) for _ in range(2)]
          e1 = A.alloc(W, F32)
          e1n = A.alloc(W, F32)
          e2 = A.alloc(W, F32)
          e3 = A.alloc(W, F32)
          dec = A.alloc(16, F32)
          qt_ = [A.alloc(W, BF16) for _ in range(2)]
          qb_ = [A.alloc(W, BF16) for _ in range(2)]
          kt_ = [A.alloc(W, BF16) for _ in range(2)]
          khat = [A.alloc(W, BF16) for _ in range(4)]
          qtT = [A.alloc(W, BF16) for _ in range(2)]
          ktT = [A.alloc(W, BF16) for _ in range(2)]
          QbT = [A.alloc(W, BF16) for _ in range(4)]
          ATm = [A.alloc(W, BF16) for _ in range(2)]
          for c in range(4):
              S.op('pool', lambda e, c=c: e.memset(QbT[c], 0.0), [], [f"QbT{c}"])
          epb = ep_bufs("C")
          hmask = C['h_mask']
          sbi = 0
          for seg in ('prev', 'own'):
              full = seg == 'own'
              eoff = NS if full else 0
              for t in range(NT):
                  b = rot("Cb", 2)
                  r0, r1 = eoff + t * 128, eoff + (t + 1) * 128
                  S.dma('sp', lf[b], s_logf[r0:r1, :], [], [f"lf{b}"], f"lf{b}")
                  S.dma('sp', kin[b], s_kin[r0:r1, :], [], [f"kin{b}"], f"kin{b}")
                  S.dma('sp', ci[b], s_ci[r0:r1, :], [], [f"ci{b}"], f"ci{b}")
                  S.op('act', lambda e, b=b: e.copy(out=lhi[b], in_=lf[b]), [f"lf{b}"], [f"lhi{b}"])
                  S.op('pool', lambda e, b=b: e.tensor_tensor(out=llo[b], in0=lf[b], in1=lhi[b], op=ALU.subtract), [f"lf{b}", f"lhi{b}"], [f"llo{b}"])
                  mats = [('h_e2', 2)] + ([('h_ecum', 0), ('h_e1', 1)] if full else [])
                  for nm, pi in mats:
                      S.op('pe', lambda e, nm=nm, pi=pi, b=b: e.matmul(psf(pi), lhsT=C[nm], rhs=lhi[b], start=True, stop=False), ["c_" + nm, f"lhi{b}"], [psk[pi]])
                      S.op('pe', lambda e, nm=nm, pi=pi, b=b: e.matmul(psf(pi), lhsT=C[nm], rhs=llo[b], start=False, stop=True), ["c_" + nm, f"llo{b}"], [psk[pi]])
                  pD = psf(3, 16)
                  for h in range(4):
                      S.op('pe', lambda e, h=h, b=b: e.matmul(pD[:, h * 4:(h + 1) * 4], lhsT=lhi[b][:, h * 128:(h + 1) * 128], rhs=C['h_indb'], start=(h == 0), stop=False, skip_group_check=True), [f"lhi{b}", "c_h_indb"], [psk[3]])
                  for h in range(4):
                      S.op('pe', lambda e, h=h, b=b: e.matmul(pD[:, h * 4:(h + 1) * 4], lhsT=llo[b][:, h * 128:(h + 1) * 128], rhs=C['h_indb'], start=False, stop=(h == 3), skip_group_check=True), [f"llo{b}", "c_h_indb"], [psk[3]])
                  S.op('act', lambda e: e.activation(out=dec, in_=pD, func=AF.Exp), [psk[3]], ["dec"])
                  S.op('act', lambda e: e.activation(out=e2, in_=psf(2), func=AF.Exp), [psk[2]], ["e2"])
                  for c in range(4):
                      S.op('dve', lambda e, c=c, b=b: e.scalar_tensor_tensor(out=khat[c], in0=kin[b], scalar=C['h_ind'][:, c:c + 1], in1=e2, op0=ALU.mult, op1=ALU.mult), [f"kin{b}", "e2", "c_h_ind"], [f"khat{c}"])
                  if full:
                      S.dma('sp', cq[b], s_cq[t * 128:(t + 1) * 128, :], [], [f"cq{b}"], f"cq{b}")
                      S.dma('sp', sz_c[b], s_sz[2][t * 128:(t + 1) * 128, :], [], [f"szc{b}"], f"szc{b}")
                      S.op('act', lambda e: e.activation(out=e3, in_=psf(0), func=AF.Exp), [psk[0]], ["e3"])
                      S.op('act', lambda e: e.activation(out=e1, in_=psf(1), func=AF.Exp), [psk[1]], ["e1"])
                      S.op('act', lambda e: e.activation(out=e1n, in_=psf(1), func=AF.Exp, scale=-1.0), [psk[1]], ["e1n"])
                      S.op('pool', lambda e, b=b: e.tensor_tensor(out=qt_[b], in0=cq[b], in1=e1, op=ALU.mult), [f"cq{b}", "e1"], [f"qt{b}"])
                      S.op('pool', lambda e, b=b: e.tensor_tensor(out=qb_[b], in0=cq[b], in1=e3, op=ALU.mult), [f"cq{b}", "e3"], [f"qb{b}"])
                      S.op('pool', lambda e, b=b: e.tensor_tensor(out=kt_[b], in0=kin[b], in1=e1n, op=ALU.mult), [f"kin{b}", "e1n"], [f"kt{b}"])
                      p0 = psb(0)[:, 0:512].rearrange("p (h t) -> p h t", h=4)
                      p1 = psb(1)[:, 0:512].rearrange("p (h t) -> p h t", h=4)
                      p4 = psb(4)[:, 0:512].rearrange("p (h t) -> p h t", h=4)
                      for h in range(4):
                          S.op('pe', lambda e, h=h, b=b: e.transpose(out=p0[:, h, :], in_=qt_[b][:, h * 128:(h + 1) * 128], identity=ident), [f"qt{b}", "c_ident", "e3"], [psk[0]])
                      for h in range(4):
                          S.op('pe', lambda e, h=h, b=b: e.transpose(out=p1[:, h, :], in_=kt_[b][:, h * 128:(h + 1) * 128], identity=ident), [f"kt{b}", "c_ident", "e1", "e1n"], [psk[1]])
                      for h in range(4):
                          S.op('pe', lambda e, h=h, b=b: e.transpose(out=p4[:, h, :], in_=qb_[b][:, h * 128:(h + 1) * 128], identity=ident), [f"qb{b}", "c_ident"], [psk[4]])
                      S.op('act', lambda e, b=b: e.copy(out=hv(qtT[b]), in_=p0), [psk[0]], [f"qtT{b}"])
                      S.op('act', lambda e, b=b: e.copy(out=hv(ktT[b]), in_=p1), [psk[1]], [f"ktT{b}"])
                      for c in range(4):
                          S.op('dve', lambda e, c=c: e.tensor_copy(out=hv(QbT[c])[:, :, 32 * c:32 * c + 32], in_=p4[:, :, 32 * c:32 * c + 32]), [psk[4]], [f"QbT{c}"])
                      pA = hv(psf(5))
                      for h in range(4):
                          S.op('pe', lambda e, h=h, b=b: e.matmul(pA[:, h, :], lhsT=hv(ktT[b])[:, h, :], rhs=hv(qtT[b])[:, h, :], start=True, stop=True, skip_group_check=True), [f"ktT{b}", f"qtT{b}"], [psk[5]])
                      S.op('dve', lambda e, b=b: e.tensor_tensor(out=hv(ATm[b]), in0=pA, in1=bc_mid(hmask, 4), op=ALU.mult), [psk[5], "c_h_mask"], [f"ATm{b}"])
                      pO = hv(psf(2))
                  for c in range(4):
                      if full:
                          for h in range(4):
                              S.op('pe', lambda e, h=h, c=c, sbi=sbi: e.matmul(pO[:, h, :], lhsT=hv(QbT[c])[:, h, :], rhs=hv(Sbf[sbi])[:, h, :], start=(c == 0 and h == 0), stop=False, skip_group_check=True),
                                   [f"QbT{c}", f"Sbf{sbi}"], [psk[2]])
                      pKV = hv(psf(3))
                      for h in range(4):
                          S.op('pe', lambda e, h=h, c=c, b=b: e.matmul(pKV[:, h, :], lhsT=khat[c][:, h * 128:(h + 1) * 128], rhs=ci[b][:, h * 128:(h + 1) * 128], start=True, stop=True, skip_group_check=True),
                               [f"khat{c}", f"ci{b}", "dec"], [psk[3]])
                      S.op('dve', lambda e, c=c: e.tensor_tensor(out=Sst3, in0=Sst3, in1=bc_last(dec.rearrange("p (h c) -> p h c", h=4)[:, :, c], 128), op=ALU.mult), ["Sst", "dec"], ["Sst"])
                      S.op('dve', lambda e, pKV=pKV: e.tensor_tensor(out=Sst3, in0=pKV, in1=Sst3, op=ALU.add), ["Sst", psk[3]], ["Sst"])
                      if (not full) and t == NT - 1 and c == 3:
                          S.op('dve', lambda e: e.tensor_scalar(out=Sst, in0=Sst, scalar1=flag, scalar2=None, op0=ALU.mult), ["Sst", "c_flag"], ["Sst"])
                      sbi = 1 - sbi
                      S.op('act', lambda e, sbi=sbi: e.copy(out=Sbf[sbi], in_=Sst), ["Sst"], [f"Sbf{sbi}"])
                  if full:
                      for h in range(4):
                          S.op('pe', lambda e, h=h, b=b: e.matmul(pO[:, h, :], lhsT=hv(ATm[b])[:, h, :], rhs=ci[b][:, h * 128:(h + 1) * 128], start=False, stop=True, skip_group_check=True), [f"ATm{b}", f"ci{b}"], [psk[2]])
                      epilogue(2, ([pO[:, h, :] for h in range(4)], [psk[2]]), t, sz_c[b], f"szc{b}", hgt, "hgt", True, epb[b], 4)
          S.barrier()
          A.pop()

          _ck('C')
          A.push()
          brA = A.alloc(16 * 512, BF16)
          brA3 = brA.rearrange("p (a t) -> p a t", a=16)
          wbr = [A.alloc(16 * 128, BF16) for _ in range(2)]
          sgt = [A.alloc(512, BF16) for _ in range(3)]
          macc = A.alloc(512, F32)
          mtm = [A.alloc(512, F32) for _ in range(2)]
          mT = A.alloc(16 * 512, BF16)
          mT3 = mT.rearrange("p (c t) -> p c t", c=16)
          wo = [A.alloc(16 * 512, BF16) for _ in range(2)]
          xres = [A.alloc(512, F32) for _ in range(2)]
          outt = [A.alloc(512, F32) for _ in range(2)]
          out_toks = []
          for tg in range(NTG):
              for n in range(4):
                  S.dma('sp', brA3[:, n * 4:(n + 1) * 4, :], s_brT[n][:, :, tg * 512:(tg + 1) * 512].rearrange("k p t -> p k t"), [], ["brA"], f"brA{n}")
              for c in range(16):
                  wi = rot("wbr", 2)
                  wv4 = wbr[wi].rearrange("p (n k m) -> p n k m", n=4, k=4)
                  for n in range(4):
                      S.dma('pool', wv4[:, n, :, :], w_br[l][n][:, c * 128:(c + 1) * 128].rearrange("(k p) m -> p k m", p=128), [], [f"wbr{wi}"], f"wbr{wi}_{n}")
                  for n in range(4):
                      pi = rot("Gps", 3)
                      for k in range(4):
                          S.op('pe', lambda e, n=n, k=k, pi=pi, wv4=wv4: e.matmul(psf(pi), lhsT=wv4[:, n, k, :], rhs=brA3[:, n * 4 + k, :], start=(k == 0), stop=(k == 3)), [f"wbr{wi}", "brA"], [psk[pi]])
                      si = rot("sgt", 3)
                      S.dma('sp', sgt[si], s_sg[n][c][:, tg * 512:(tg + 1) * 512], [], [f"sgt{si}"], f"sgt{si}")
                      if n == 0:
                          S.op('dve', lambda e, pi=pi, si=si: e.tensor_tensor(out=macc, in0=psf(pi), in1=sgt[si], op=ALU.mult), [psk[pi], f"sgt{si}"], ["macc"])
                      else:
                          mi = rot("mtm", 2)
                          S.op('dve', lambda e, pi=pi, si=si, mi=mi: e.tensor_tensor(out=mtm[mi], in0=psf(pi), in1=sgt[si], op=ALU.mult), [psk[pi], f"sgt{si}"], [f"mtm{mi}"])
                          if n < 3:
                              S.op('pool', lambda e, mi=mi: e.tensor_tensor(out=macc, in0=macc, in1=mtm[mi], op=ALU.add), ["macc", f"mtm{mi}"], ["macc"])
                          else:
                              S.op('pool', lambda e, mi=mi, c=c: e.tensor_tensor(out=mT3[:, c, :], in0=macc, in1=mtm[mi], op=ALU.add), ["macc", f"mtm{mi}"], ["mT"])
              for cg in range(4):
                  wi = rot("wo", 2)
                  wo3 = wo[wi].rearrange("p (c n) -> p c n", c=16)
                  S.dma('pool', wo3, w_out[l][:, cg * 512:(cg + 1) * 512].rearrange("(c p) n -> p c n", p=128), [], [f"wo{wi}"], f"wo{wi}")
                  for tt in range(4):
                      pi = 3 + rot("Gpo", 3)
                      row0 = (tg * 4 + tt) * 128
                      for c in range(16):
                          S.op('pe', lambda e, c=c, tt=tt, pi=pi, wo3=wo3: e.matmul(psf(pi), lhsT=mT3[:, c, tt * 128:(tt + 1) * 128], rhs=wo3[:, c, :], start=(c == 0), stop=(c == 15)), ["mT", f"wo{wi}"], [psk[pi]])
                      xi = rot("xres", 2)
                      S.dma('sp', xres[xi], xown[row0:row0 + 128, cg * 512:(cg + 1) * 512], [], [f"xres{xi}"], f"xres{xi}")
                      S.op('dve', lambda e, pi=pi, xi=xi: e.tensor_tensor(out=outt[xi], in0=psf(pi), in1=xres[xi], op=ALU.add), [psk[pi], f"xres{xi}"], [f"outt{xi}"])
                      out_toks.append(S.dma('sp', xdst[row0:row0 + 128, cg * 512:(cg + 1) * 512], outt[xi], [f"outt{xi}"], [], f"outt{xi}"))
          S.barrier()
          A.pop()
          if not last and gather:
              tokc = [None]

              def cc(e):
                  return e.collective_compute(mybir.CollectiveComputeKind.AllGather if hasattr(mybir, "CollectiveComputeKind") else "AllGather",
                                              ALU.bypass, replica_groups=[[2 * i, 2 * i + 1] for i in range(ncores // 2)], ins=[x1[:, :]], outs=[x1g[:, :]])
              waits = S._waits('pool', {})
              s = [S._newsem("cc"), 0]
              s[1] += 16
              tok = (s[0], s[1])
              S.prog['pool'].append((waits, cc, tok, 16))
              S._commit(tok, [], ["x1g"])
              S.barrier()
          A.pop()
    except _Stop:
        pass
    S.barrier()
    S.emit()
    return nc


_CACHE = {}


def _core_inputs(NT, L, layer_ids, xe, mem_b, g, P):
    c = host_consts(NT, g)
    m = {"xe": np.ascontiguousarray(xe, dtype=np.float32), "mem": np.ascontiguousarray(mem_b)}
    sel = np.array(layer_ids)
    m["norm_g"] = np.ascontiguousarray(P['norm_g'][sel])
    m["w_in"] = np.ascontiguousarray(P['w_in'][sel])
    m["qk_g"] = np.ascontiguousarray(P['qk_norm_g'][sel])
    m["ret_g"] = np.ascontiguousarray(P['ret_norm_g'][sel])
    m["hgrn_g"] = np.ascontiguousarray(P['hgrn_norm_g'][sel])
    m["lb_logits"] = np.ascontiguousarray(P['lb_logits'])
    m["lsel"] = np.ascontiguousarray(np.broadcast_to((sel > 0).astype(np.float32)[None, :], (128, len(layer_ids))))
    m["w_mem_kv"] = np.ascontiguousarray(P['w_mem_kv'][sel])
    m["w_branch"] = np.ascontiguousarray(P['w_branch'][sel])
    m["w_out"] = np.ascontiguousarray(P['w_out'][sel])
    for k in CONST_SPECS:
        if k == 'flag':
            m["c_flag"] = np.full((128, 1), float(g), np.float32)
        else:
            m["c_" + k] = np.ascontiguousarray(c[k])
    m["c_cos"] = c['cos']
    m["c_sin"] = c['sin']
    return m


def run_model(x, mem, P, fused=True, debug_outs=()):
    B, SEQ, _ = x.shape
    NS = SEQ // 2
    NT = NS // 128
    ncores = 2 * B
    x = np.asarray(x, np.float32)
    outs = None
    if fused:
        key = (NT, 2, ncores, tuple(debug_outs))
        if key not in _CACHE:
            _CACHE[key] = build(NT, 2, ncores=ncores, debug_outs=debug_outs)
        nc = _CACHE[key]
        maps = []
        for core in range(ncores):
            b, g = core // 2, core % 2
            prev = x[b, 0:NS] if g == 1 else np.zeros((NS, D), np.float32)
            xe = np.concatenate([prev, x[b, g * NS:(g + 1) * NS]], axis=0)
            maps.append(_core_inputs(NT, 2, [0, 1], xe, mem[b], g, P))
        res = run_bass_kernel_spmd(nc, maps, core_ids=list(range(ncores)))
        outs = res.results
    else:
        cur = x
        for l in range(2):
            key = (NT, 1, ncores, tuple(debug_outs))
            if key not in _CACHE:
                _CACHE[key] = build(NT, 1, ncores=ncores, debug_outs=debug_outs)
            nc = _CACHE[key]
            maps = []
            for core in range(ncores):
                b, g = core // 2, core % 2
                prev = cur[b, 0:NS] if g == 1 else np.zeros((NS, D), np.float32)
                xe = np.concatenate([prev, cur[b, g * NS:(g + 1) * NS]], axis=0)
                maps.append(_core_inputs(NT, 1, [l], xe, mem[b], g, P))
            res = run_bass_kernel_spmd(nc, maps, core_ids=list(range(ncores)))
            outs = res.results
            nxt = np.empty_like(cur)
            for core in range(ncores):
                b, g = core // 2, core % 2
                nxt[b, g * NS:(g + 1) * NS] = outs[core]["y"]
            cur = nxt
    out = np.empty((B, SEQ, D), np.float32)
    for core in range(ncores):
        b, g = core // 2, core % 2
        out[b, g * NS:(g + 1) * NS] = outs[core]["y"]
    return out, outs


FUSED = False


def kernel(x, mem, norm_g, w_in, qk_norm_g, ret_norm_g, hgrn_norm_g, lb_logits, w_mem_kv, w_branch, w_out):
    P = dict(norm_g=np.asarray(norm_g), w_in=np.asarray(w_in), qk_norm_g=np.asarray(qk_norm_g),
             ret_norm_g=np.asarray(ret_norm_g), hgrn_norm_g=np.asarray(hgrn_norm_g), lb_logits=np.asarray(lb_logits),
             w_mem_kv=np.asarray(w_mem_kv), w_branch=np.asarray(w_branch), w_out=np.asarray(w_out))
    out, _ = run_model(np.asarray(x), np.asarray(mem), P, fused=FUSED)
    return out
```
